# Optimizing a Trainium2 kernel written in Bass

```python
import jax, jax.numpy as jnp
from jax import lax
import numpy as np

D_MODEL = 2048
BATCH = 4
SEQ = 2048
DEPTH = 2

N_MEM = 256
MIX_WIDTH = D_MODEL
GM_WIDTH = MIX_WIDTH // 2
GM_HEAD_DIM = 128
GM_HEADS = GM_WIDTH // GM_HEAD_DIM
GM_CHUNK = 128
DN_WIDTH = MIX_WIDTH - GM_WIDTH
DN_HEAD_DIM = 128
DN_HEADS = DN_WIDTH // DN_HEAD_DIM
DN_CHUNK = 64
CONV_WIDTH = 4
XA_HEADS = 4
XA_HEAD_DIM = D_MODEL // XA_HEADS
D_FF = 5632
NORM_EPS = 1e-6
IN_COLS = 2 * GM_WIDTH + 4 * DN_WIDTH + 2 * DN_HEADS

kernel_name = "hybrid_sgu_deltanet_macaron_block"


def rmsnorm(x, g):
    xf = x.astype(jnp.float32)
    y = xf * lax.rsqrt(jnp.mean(xf * xf, axis=-1, keepdims=True) + NORM_EPS)
    return (y * g.astype(jnp.float32)).astype(x.dtype)


def l2norm(x):
    xf = x.astype(jnp.float32)
    return xf * lax.rsqrt(jnp.sum(xf * xf, axis=-1, keepdims=True) + NORM_EPS)


def swiglu_ffn(h, w_gate_up, w_down):
    gate, up = jnp.split(h @ w_gate_up, 2, axis=-1)
    return (jax.nn.silu(gate) * up) @ w_down


def causal_short_conv(x, w):
    k_width = w.shape[0]
    t_len = x.shape[1]
    xp = jnp.pad(x, ((0, 0), (k_width - 1, 0), (0, 0)))
    out = xp[:, 0:t_len] * w[0]
    for i in range(1, k_width):
        out = out + xp[:, i:i + t_len] * w[i]
    return out


def chunk_spatial_gating(u, v, sm_w, sm_b, ln_g, ln_b):
    b_, t_len, _ = u.shape
    n_chunks = t_len // GM_CHUNK
    u = jax.nn.gelu(u, approximate=False)
    vf = jax.nn.gelu(v, approximate=False).astype(jnp.float32)
    mu = jnp.mean(vf, axis=-1, keepdims=True)
    var = jnp.mean(jnp.square(vf - mu), axis=-1, keepdims=True)
    vn = ((vf - mu) * lax.rsqrt(var + NORM_EPS) * ln_g + ln_b).astype(u.dtype)
    vc = vn.reshape(b_, n_chunks, GM_CHUNK, GM_HEADS, GM_HEAD_DIM)
    causal = jnp.tril(jnp.ones((GM_CHUNK, GM_CHUNK), dtype=bool))
    w_masked = jnp.where(causal, sm_w, 0)
    mixed = jnp.einsum('hts,bnshd->bnthd', w_masked, vc) + sm_b.T[None, None, :, :, None]
    uc = u.reshape(b_, n_chunks, GM_CHUNK, GM_HEADS, GM_HEAD_DIM)
    return (uc * mixed).reshape(b_, t_len, GM_WIDTH)


def gated_delta_rule(q, k, v, g, beta):
    b_, t_len, h_, dk = q.shape
    dv = v.shape[-1]
    n_chunks = t_len // DN_CHUNK
    c = DN_CHUNK

    def to_chunks(a):
        return a.reshape(b_, n_chunks, c, h_, -1).transpose(1, 0, 3, 2, 4)

    qc = to_chunks(l2norm(q) * (dk ** -0.5))
    kc = to_chunks(l2norm(k))
    vc = to_chunks(v.astype(jnp.float32))
    gc = g.astype(jnp.float32).reshape(b_, n_chunks, c, h_).transpose(1, 0, 3, 2)
    bc = beta.astype(jnp.float32).reshape(b_, n_chunks, c, h_).transpose(1, 0, 3, 2)
    gcum = jnp.cumsum(gc, axis=-1)

    causal = jnp.tril(jnp.ones((c, c), dtype=bool))
    strict = jnp.tril(jnp.ones((c, c), dtype=bool), k=-1)
    decay = jnp.exp(jnp.where(causal, gcum[..., :, None] - gcum[..., None, :], -jnp.inf))

    k_beta = kc * bc[..., None]
    kkt = jnp.einsum('nbhtd,nbhsd->nbhts', k_beta, kc) * decay
    a_mat = jnp.eye(c, dtype=jnp.float32) + jnp.where(strict, kkt, 0.0)
    rhs = jnp.concatenate([vc * bc[..., None], k_beta * jnp.exp(gcum)[..., None]], axis=-1)
    sol = lax.linalg.triangular_solve(a_mat, rhs, left_side=True, lower=True, unit_diagonal=True)
    u_c, w_c = sol[..., :dv], sol[..., dv:]
    qk_intra = jnp.where(causal, jnp.einsum('nbhtd,nbhsd->nbhts', qc, kc) * decay, 0.0)

    def chunk_step(state, inp):
        q_i, k_i, u_i, w_i, g_i, a_i = inp
        v_new = u_i - jnp.einsum('bhck,bhkv->bhcv', w_i, state)
        o_i = (jnp.einsum('bhck,bhkv->bhcv', q_i * jnp.exp(g_i)[..., None], state)
               + jnp.einsum('bhts,bhsv->bhtv', a_i, v_new))
        g_last = g_i[..., -1]
        k_dec = k_i * jnp.exp(g_last[..., None] - g_i)[..., None]
        state = state * jnp.exp(g_last)[..., None, None] + jnp.einsum('bhck,bhcv->bhkv', k_dec, v_new)
        return state, o_i

    s0 = jnp.zeros((b_, h_, dk, dv), jnp.float32)
    _, o = lax.scan(chunk_step, s0, (qc, kc, u_c, w_c, gcum, qk_intra))
    return o.transpose(1, 0, 3, 2, 4).reshape(b_, t_len, h_, dv)


def token_mix(h, w_in, conv_w, a_log, dt_bias, sm_w, sm_b, sm_ln_g, sm_ln_b, dn_norm_w, w_out):
    b_, t_len, _ = h.shape
    proj = h @ w_in
    splits = [GM_WIDTH, 2 * GM_WIDTH, 2 * GM_WIDTH + 3 * DN_WIDTH,
              2 * GM_WIDTH + 4 * DN_WIDTH, 2 * GM_WIDTH + 4 * DN_WIDTH + DN_HEADS]
    u_a, v_a, qkv, z, b_raw, a_raw = jnp.split(proj, splits, axis=-1)

    y_a = chunk_spatial_gating(u_a, v_a, sm_w, sm_b, sm_ln_g, sm_ln_b)

    qkv = jax.nn.silu(causal_short_conv(qkv, conv_w))
    q, k, v = jnp.split(qkv, 3, axis=-1)
    q = q.reshape(b_, t_len, DN_HEADS, DN_HEAD_DIM)
    k = k.reshape(b_, t_len, DN_HEADS, DN_HEAD_DIM)
    v = v.reshape(b_, t_len, DN_HEADS, DN_HEAD_DIM)
    beta = jax.nn.sigmoid(b_raw.astype(jnp.float32))
    g = -jnp.exp(a_log.astype(jnp.float32)) * jax.nn.softplus(a_raw.astype(jnp.float32) + dt_bias.astype(jnp.float32))
    o = gated_delta_rule(q, k, v, g, beta)
    zf = z.astype(jnp.float32).reshape(b_, t_len, DN_HEADS, DN_HEAD_DIM)
    o = rmsnorm(o, dn_norm_w) * jax.nn.silu(zf)
    y_b = o.reshape(b_, t_len, DN_WIDTH).astype(h.dtype)

    return jnp.concatenate([y_a, y_b], axis=-1) @ w_out


def cross_attend(h, mem_h, w_xq, w_xkv, w_xo):
    b_, t_len, _ = h.shape
    m_len = mem_h.shape[1]
    q = (h @ w_xq).reshape(b_, t_len, XA_HEADS, XA_HEAD_DIM)
    kv = (mem_h @ w_xkv).reshape(b_, m_len, 2, XA_HEADS, XA_HEAD_DIM)
    k, v = kv[:, :, 0], kv[:, :, 1]
    s = jnp.einsum('bthd,bmhd->bhtm', q, k).astype(jnp.float32) * (XA_HEAD_DIM ** -0.5)
    p = jax.nn.softmax(s, axis=-1).astype(v.dtype)
    o = jnp.einsum('bhtm,bmhd->bthd', p, v).reshape(b_, t_len, XA_HEADS * XA_HEAD_DIM)
    return o @ w_xo


def setup_inputs(seed: int = 0) -> dict:
    key = jax.random.key(seed)
    ks = iter(jax.random.split(key, 64))
    L = DEPTH

    def dense(shape, fan_in):
        return jax.random.normal(next(ks), shape, jnp.float32) * (fan_in ** -0.5)

    def gain(shape):
        return 1.0 + 0.05 * jax.random.normal(next(ks), shape, jnp.float32)

    def small(shape):
        return 0.02 * jax.random.normal(next(ks), shape, jnp.float32)

    x = jax.random.normal(next(ks), (BATCH, SEQ, D_MODEL), jnp.float32)
    mem = jax.random.normal(next(ks), (BATCH, N_MEM, D_MODEL), jnp.float32)
    a_log = jnp.log(jax.random.uniform(next(ks), (L, DN_HEADS), jnp.float32, minval=1.0, maxval=16.0))
    dt = jnp.exp(jax.random.uniform(next(ks), (L, DN_HEADS), jnp.float32,
                                    minval=float(np.log(1e-3)), maxval=float(np.log(1e-1))))
    dt_bias = dt + jnp.log(-jnp.expm1(-dt))
    return {
        "x": x,
        "mem": mem,
        "ffn1_norm_pre": gain((L, D_MODEL)),
        "ffn1_w_gate_up": dense((L, D_MODEL, 2 * D_FF), D_MODEL),
        "ffn1_w_down": dense((L, D_FF, D_MODEL), D_FF),
        "ffn1_norm_post": gain((L, D_MODEL)),
        "mix_norm_pre": gain((L, D_MODEL)),
        "w_in": dense((L, D_MODEL, IN_COLS), D_MODEL),
        "conv_w": dense((L, CONV_WIDTH, 3 * DN_WIDTH), CONV_WIDTH),
        "a_log": a_log,
        "dt_bias": dt_bias,
        "sm_w": dense((L, GM_HEADS, GM_CHUNK, GM_CHUNK), GM_CHUNK),
        "sm_b": gain((L, GM_HEADS, GM_CHUNK)),
        "sm_ln_g": gain((L, GM_WIDTH)),
        "sm_ln_b": small((L, GM_WIDTH)),
        "dn_norm_w": gain((L, DN_HEAD_DIM)),
        "w_out": dense((L, MIX_WIDTH, D_MODEL), MIX_WIDTH),
        "mix_norm_post": gain((L, D_MODEL)),
        "xa_norm_pre": gain((L, D_MODEL)),
        "mem_norm": gain((L, D_MODEL)),
        "w_xq": dense((L, D_MODEL, D_MODEL), D_MODEL),
        "w_xkv": dense((L, D_MODEL, 2 * D_MODEL), D_MODEL),
        "w_xo": dense((L, D_MODEL, D_MODEL), D_MODEL),
        "xa_norm_post": gain((L, D_MODEL)),
        "ffn2_norm_pre": gain((L, D_MODEL)),
        "ffn2_w_gate_up": dense((L, D_MODEL, 2 * D_FF), D_MODEL),
        "ffn2_w_down": dense((L, D_FF, D_MODEL), D_FF),
        "ffn2_norm_post": gain((L, D_MODEL)),
    }


def reference(x, mem, ffn1_norm_pre, ffn1_w_gate_up, ffn1_w_down, ffn1_norm_post,
              mix_norm_pre, w_in, conv_w, a_log, dt_bias, sm_w, sm_b, sm_ln_g, sm_ln_b,
              dn_norm_w, w_out, mix_norm_post, xa_norm_pre, mem_norm, w_xq, w_xkv, w_xo,
              xa_norm_post, ffn2_norm_pre, ffn2_w_gate_up, ffn2_w_down, ffn2_norm_post):
    for l in range(DEPTH):
        f = swiglu_ffn(rmsnorm(x, ffn1_norm_pre[l]), ffn1_w_gate_up[l], ffn1_w_down[l])
        x = x + 0.5 * rmsnorm(f, ffn1_norm_post[l])
        m = token_mix(rmsnorm(x, mix_norm_pre[l]), w_in[l], conv_w[l], a_log[l], dt_bias[l],
                      sm_w[l], sm_b[l], sm_ln_g[l], sm_ln_b[l], dn_norm_w[l], w_out[l])
        x = x + rmsnorm(m, mix_norm_post[l])
        c = cross_attend(rmsnorm(x, xa_norm_pre[l]), rmsnorm(mem, mem_norm[l]), w_xq[l], w_xkv[l], w_xo[l])
        x = x + rmsnorm(c, xa_norm_post[l])
        f = swiglu_ffn(rmsnorm(x, ffn2_norm_pre[l]), ffn2_w_gate_up[l], ffn2_w_down[l])
        x = x + 0.5 * rmsnorm(f, ffn2_norm_post[l])
    return x
```

```python
import numpy as np
from contextlib import ExitStack
import concourse.bass as bass
import concourse.mybir as mybir
from concourse.bass_utils import run_bass_kernel_spmd

F32 = mybir.dt.float32
F32R = mybir.dt.float32
AF = mybir.ActivationFunctionType
ALU = mybir.AluOpType
AX = mybir.AxisListType

NCORES = 8
D_MODEL = 2048
BATCH = 4
SEQ = 2048
DEPTH = 2
N_MEM = 256
GM_WIDTH = 1024
GM_HEADS = 8
DN_WIDTH = 1024
DN_HEADS = 8
XA_HEADS = 4
D_FF = 5632
EPS = 1e-6
IN_COLS = 2 * GM_WIDTH + 4 * DN_WIDTH + 2 * DN_HEADS
NTOK = BATCH * SEQ // NCORES


class _Op:
    __slots__ = ("queue", "lane", "idx", "fn", "waits", "signal", "snap", "is_dma")

    def __init__(self, queue, lane, idx, fn, is_dma):
        self.queue = queue
        self.lane = lane
        self.idx = idx
        self.fn = fn
        self.waits = []
        self.signal = False
        self.snap = None
        self.is_dma = is_dma


class Tracker:
    def __init__(self):
        self.queues = {q: [] for q in ("sp", "act", "dve", "pool", "pe")}
        self.lanes = {}
        self.seen = {q: {} for q in self.queues}
        self.last_writer = {}
        self.readers = {}

    def op(self, queue, fn, reads=(), writes=(), dma_lane=None):
        is_dma = dma_lane is not None
        lane = dma_lane if is_dma else queue
        lops = self.lanes.setdefault(lane, [])
        o = _Op(queue, lane, len(lops), fn, is_dma)
        deps = {}

        def add(d):
            if deps.get(d[0], -1) < d[1]:
                deps[d[0]] = d[1]

        for t in reads:
            w = self.last_writer.get(t)
            if w is not None:
                add(w)
        for t in writes:
            w = self.last_writer.get(t)
            if w is not None:
                add(w)
            for r in self.readers.get(t, ()):
                add(r)
        if is_dma and o.idx > 0:
            add((lane, o.idx - 1))
        seen = self.seen[queue]
        for l, i in sorted(deps.items(), key=lambda kv: str(kv[0])):
            if l == "pe" and queue == "pe" and not is_dma:
                continue
            if seen.get(l, -1) >= i:
                continue
            o.waits.append((l, i))
            dep = self.lanes[l][i]
            dep.signal = True
            seen[l] = i
            for l2, i2 in dep.snap.items():
                if seen.get(l2, -1) < i2:
                    seen[l2] = i2
        o.snap = dict(seen)
        if is_dma:
            o.signal = True
        lops.append(o)
        self.queues[queue].append(o)
        for t in reads:
            self.readers.setdefault(t, []).append((lane, o.idx))
        for t in writes:
            self.last_writer[t] = (lane, o.idx)
            self.readers[t] = []
        return o

    def wait_all(self, queue, tokens):
        return self.op(queue, None, reads=(), writes=tuple(tokens))

    def barrier(self):
        lasts = []
        for l, ops in self.lanes.items():
            for o in reversed(ops):
                if o.fn is not None:
                    lasts.append((l, o.idx))
                    break
        if not lasts:
            return
        for i, (l, idx) in enumerate(lasts):
            self.last_writer[("__bar", i)] = (l, idx)
            self.readers[("__bar", i)] = []
        toks = [("__bar", i) for i in range(len(lasts))]
        for q in self.queues:
            self.op(q, None, reads=toks)
        self.last_writer = {}
        self.readers = {}

    def emit(self, nc, stack):
        sems = {}
        for lane in self.lanes:
            sems[lane] = stack.enter_context(nc.semaphore("s_" + str(lane)))
        val = {}
        for lane, lops in self.lanes.items():
            c = 0
            for o in lops:
                if o.is_dma:
                    c += 16
                    val[(lane, o.idx)] = c
                elif o.signal:
                    c += 1
                    val[(lane, o.idx)] = c
        block = stack.enter_context(nc.Block())
        qmap = {"sp": block.sync, "act": block.scalar, "dve": block.vector,
                "pool": block.gpsimd, "pe": block.tensor}
        for q, reg in qmap.items():
            ops = self.queues[q]

            def body(eng, ops=ops):
                for o in ops:
                    for l, i in o.waits:
                        eng.wait_ge(sems[l], val[(l, i)])
                    if o.fn is None:
                        continue
                    ins = o.fn(eng)
                    if o.signal:
                        ins.then_inc(sems[o.lane], 16 if o.is_dma else 1)

            reg(body)


ARENA_WORDS = 53000


class Ctx:
    def __init__(self, r_words=53100, f_words=16):
        self.r_words, self.f_words = r_words, f_words
        self.nc = bass.Bass("TRN2", target_bir_lowering=False)
        self.T = Tracker()
        self.st = ExitStack()
        self.out_tokens = []
        self.arena_f = self.st.enter_context(self.nc.sbuf_tensor("arena_f", [128, f_words], F32))
        self.arena_r = self.st.enter_context(self.nc.sbuf_tensor("arena_r", [128, r_words], F32R))
        self.off_r = 0
        self.parena = self.st.enter_context(self.nc.psum_tensor("parena", [128, 4096], F32))
        self.off = 0
        self.poff = 0
        self.pfx = ""
        self.io = {}
        self.ext_in = []

    def begin_stage(self, pfx, io=None):
        self.T.barrier()
        self.off = 0
        self.off_r = 0
        self.poff = 0
        self.pfx = pfx
        self.io = dict(io or {})

    def dram_in(self, name, shape, dt=F32):
        if name in self.io:
            return self.io[name]
        self.ext_in.append(self.pfx + name)
        return self.nc.dram_tensor(self.pfx + name, list(shape), dt, kind="ExternalInput").ap()

    def dram_out(self, name, shape, dt=F32):
        if name in self.io:
            return self.io[name]
        return self.nc.dram_tensor(self.pfx + name, list(shape), dt, kind="ExternalOutput").ap()

    def dram_tmp(self, name, shape, dt=F32):
        return self.nc.dram_tensor(name, list(shape), dt, kind="Internal").ap()

    def sb(self, name, shape, dt=F32):
        shape = list(shape)
        n = 1
        for d in shape[1:]:
            n *= d
        n2 = n + (n % 2)
        if dt == F32R:
            assert self.off_r + n2 <= self.r_words, (name, shape, self.off_r)
            v = self.arena_r[0:shape[0], self.off_r:self.off_r + n]
            self.off_r += n2
        else:
            assert dt == F32
            assert self.off + n2 <= self.f_words, (name, shape, self.off)
            v = self.arena_f[0:shape[0], self.off:self.off + n]
            self.off += n2
        if len(shape) == 3:
            v = v.rearrange("p (a b) -> p a b", b=shape[2])
        return v

    def ps(self, name, shape=(128, 512), dt=F32):
        n = shape[1]
        n2 = ((n + 511) // 512) * 512
        assert self.poff + n2 <= 4096, (name, self.poff)
        v = self.parena[0:shape[0], self.poff:self.poff + n]
        self.poff += n2
        return v

    def finish(self):
        self.T.barrier()
        self.T.emit(self.nc, self.st)
        self.st.close()
        return self.nc


class Rot:
    def __init__(self, bufs, name):
        self.bufs = bufs
        self.name = name
        self.i = 0

    def next(self):
        s = self.i % len(self.bufs)
        self.i += 1
        return s, self.bufs[s], (self.name, s)


def gather_rows(C, name, shard_ext, rows, cols):
    T = C.T
    sh = C.dram_tmp(name + "_shard", [rows, cols])
    full = C.dram_tmp(name + "_full", [NCORES * rows, cols])
    T.op("pool", lambda e: e.dma_start(out=sh, in_=shard_ext), writes=[name + "_sh"], dma_lane="g_" + name)
    T.op("pool", lambda e: e.collective_compute("AllGather", ALU.bypass, replica_groups=[list(range(NCORES))],
                                                ins=[sh], outs=[full]),
         reads=[name + "_sh"], writes=[name + "_full"], dma_lane="cc_" + name)
    return full


def make_ones(C):
    ones32 = C.sb("ones32", [128, 128], F32)
    ones = C.sb("ones", [128, 128], F32R)
    C.T.op("dve", lambda e: e.memset(ones32[:], 1.0), writes=["ones32"])
    C.T.op("dve", lambda e: e.tensor_copy(out=ones[:], in_=ones32[:]), reads=["ones32"], writes=["ones"])
    return ones


def emit_rstd(C, ones, src_fn, src_tokens, nch, width, dim, psn, sq_rot, rstd, tmp, tag):
    T = C.T
    for c in range(nch):
        s, sq, sqt = sq_rot.next()
        T.op("act", lambda e, c=c, sq=sq: e.activation(out=sq[:, :width], in_=src_fn(c), func=AF.Square),
             reads=[src_tokens[c]], writes=[sqt])
        T.op("pe", lambda e, c=c, sq=sq: e.matmul(psn[:, :width], lhsT=ones[:], rhs=sq[:, :width],
                                                  start=(c == 0), stop=(c == nch - 1)),
             reads=[sqt, "ones"], writes=[("psn", tag)])
    T.op("dve", lambda e: e.tensor_scalar(out=tmp[:, :width], in0=psn[:, :width], scalar1=1.0 / dim, scalar2=EPS,
                                          op0=ALU.mult, op1=ALU.add),
         reads=[("psn", tag)], writes=[("tmp", tag)])
    T.op("act", lambda e: e.activation(out=tmp[:, :width], in_=tmp[:, :width], func=AF.Sqrt),
         reads=[("tmp", tag)], writes=[("tmp", tag)])
    T.op("dve", lambda e: e.reciprocal(out=rstd[:, :width], in_=tmp[:, :width]),
         reads=[("tmp", tag)], writes=[("rstd", tag)])


def emit_lin(C, NT, DIN, DOUT, pre_norm, epi="none", post=None, TT=512, NB=3):
    T = C.T
    KC = DIN // 128
    OCW = DOUT // 128
    OC = OCW // 2 if epi == "swiglu" else OCW
    NTT = NT // TT
    x = C.dram_in("x", [DIN, NT])
    w = C.dram_in("w", [OCW, 128, KC * 128])
    if pre_norm:
        g = C.dram_in("g", [128, KC])
    if post is not None:
        pg = C.dram_in("pg", [128, OC])
        res = C.dram_in("res", [OC * 128, NT])
    y = C.dram_out("y", [OC * 128, NT])

    ones = make_ones(C)
    xn = C.sb("xn", [128, KC, TT], F32R)
    if pre_norm:
        xin = C.sb("xin", [128, KC, TT], F32)
        gt = C.sb("gt", [128, KC], F32)
        T.op("sp", lambda e: e.dma_start(out=gt[:], in_=g), writes=["gt"], dma_lane="dg")
    if post is not None:
        acc = C.sb("acc", [128, OC, TT], F32)
        pgt = C.sb("pgt", [128, OC], F32)
        T.op("sp", lambda e: e.dma_start(out=pgt[:], in_=pg), writes=["pgt"], dma_lane="dpg")
        rrot = Rot([C.sb(f"rb{i}", [128, TT], F32) for i in range(2)], "rb")
    if pre_norm or post is not None:
        sq_rot = Rot([C.sb(f"sq{i}", [128, TT], F32R) for i in range(2)], "sq")
        rstd = C.sb("rstd", [128, TT], F32)
        tmp = C.sb("tmp", [128, TT], F32)
        psn = C.ps("psn")
    wrot = Rot([C.sb(f"wb{i}", [128, KC * 128], F32R) for i in range(NB)], "w")
    orot = Rot([C.sb(f"ob{i}", [128, TT], F32) for i in range(3)], "o")
    o2rot = Rot([C.sb(f"oc{i}", [128, TT], F32) for i in range(2)], "o2")
    prot = Rot([C.ps(f"ps{i}") for i in range(6)], "ps")

    xv = x.rearrange("(kc p) t -> p kc t", p=128)
    for tt in range(NTT):
        tsl = slice(tt * TT, (tt + 1) * TT)
        if pre_norm:
            T.op("sp", lambda e, tsl=tsl: e.dma_start(out=xin[:], in_=xv[:, :, tsl]),
                 writes=[("xin", k) for k in range(KC)], dma_lane="dx")
            emit_rstd(C, ones, lambda c: xin[:, c, :], [("xin", k) for k in range(KC)], KC, TT, DIN,
                      psn, sq_rot, rstd, tmp, "pre")
            for k in range(KC):
                T.op("dve", lambda e, k=k: e.scalar_tensor_tensor(out=xn[:, k, :], in0=xin[:, k, :],
                                                                   scalar=gt[:, k:k + 1], in1=rstd[:],
                                                                   op0=ALU.mult, op1=ALU.mult),
                     reads=[("xin", k), "gt", ("rstd", "pre")], writes=[("xn", k)])
        else:
            step = max(1, 8 // 1)
            for k0 in range(0, KC, step):
                k1 = min(KC, k0 + step)
                T.op("pool", lambda e, k0=k0, k1=k1, tsl=tsl: e.dma_start(out=xn[:, k0:k1, :], in_=xv[:, k0:k1, tsl]),
                     writes=[("xn", k) for k in range(k0, k1)], dma_lane="dx")

        def load_w(widx):
            s, wb, wt = wrot.next()
            T.op("pool", lambda e, wb=wb, widx=widx: e.dma_start(out=wb[:], in_=w[widx], max_dma_last_dim=8192),
                 writes=[wt], dma_lane=f"dw{s}")
            return wb, wt

        def mm(wb, wt):
            s, p, pt = prot.next()
            for k in range(KC):
                T.op("pe", lambda e, k=k, p=p, wb=wb: e.matmul(p[:, :TT], lhsT=wb[:, k * 128:(k + 1) * 128],
                                                                rhs=xn[:, k, :], start=(k == 0), stop=(k == KC - 1)),
                     reads=[wt, ("xn", k)], writes=[pt])
            return p, pt

        for oc in range(OC):
            if epi == "swiglu":
                wg, wgt = load_w(oc)
                wu, wut = load_w(oc + OC)
                pg_, pgt_ = mm(wg, wgt)
                pu_, put_ = mm(wu, wut)
                s, ob, ot = orot.next()
                T.op("act", lambda e, ob=ob, pg_=pg_: e.activation(out=ob[:], in_=pg_[:, :TT], func=AF.Silu),
                     reads=[pgt_], writes=[ot])
                s2, ob2, ot2 = o2rot.next()
                T.op("dve", lambda e, ob=ob, ob2=ob2, pu_=pu_: e.tensor_tensor(out=ob2[:], in0=ob[:], in1=pu_[:, :TT],
                                                                                 op=ALU.mult),
                     reads=[ot, put_], writes=[ot2])
                T.op("sp", lambda e, ob2=ob2, oc=oc, tsl=tsl: e.dma_start(out=y[oc * 128:(oc + 1) * 128, tsl], in_=ob2[:]),
                     reads=[ot2], dma_lane=f"do{s2}")
                C.out_tokens.append(ot2)
            else:
                wb, wt = load_w(oc)
                p, pt = mm(wb, wt)
                if post is None:
                    s, ob, ot = orot.next()
                    eng = "act" if oc % 2 == 0 else "dve"
                    if eng == "act":
                        T.op("act", lambda e, ob=ob, p=p: e.copy(out=ob[:], in_=p[:, :TT]), reads=[pt], writes=[ot])
                    else:
                        T.op("dve", lambda e, ob=ob, p=p: e.tensor_copy(out=ob[:], in_=p[:, :TT]), reads=[pt], writes=[ot])
                    T.op("sp", lambda e, ob=ob, oc=oc, tsl=tsl: e.dma_start(out=y[oc * 128:(oc + 1) * 128, tsl], in_=ob[:]),
                         reads=[ot], dma_lane=f"do{s}")
                    C.out_tokens.append(ot)
                else:
                    T.op("dve", lambda e, oc=oc, p=p: e.tensor_copy(out=acc[:, oc, :], in_=p[:, :TT]),
                         reads=[pt], writes=[("acc", oc)])
        if post is not None:
            emit_rstd(C, ones, lambda c: acc[:, c, :], [("acc", k) for k in range(OC)], OC, TT, OC * 128,
                      psn, sq_rot, rstd, tmp, "post")
            for oc in range(OC):
                sr, rb, rt = rrot.next()
                T.op("sp", lambda e, rb=rb, oc=oc, tsl=tsl: e.dma_start(out=rb[:], in_=res[oc * 128:(oc + 1) * 128, tsl]),
                     writes=[rt], dma_lane=f"dr{sr}")
                s, ob, ot = orot.next()
                T.op("dve", lambda e, ob=ob, oc=oc: e.scalar_tensor_tensor(out=ob[:], in0=acc[:, oc, :],
                                                                            scalar=pgt[:, oc:oc + 1], in1=rstd[:],
                                                                            op0=ALU.mult, op1=ALU.mult),
                     reads=[("acc", oc), "pgt", ("rstd", "post")], writes=[ot])
                s2, ob2, ot2 = o2rot.next()
                T.op("dve", lambda e, ob=ob, ob2=ob2, rb=rb: e.scalar_tensor_tensor(out=ob2[:], in0=ob[:], scalar=float(post),
                                                                                     in1=rb[:], op0=ALU.mult, op1=ALU.add),
                     reads=[ot, rt], writes=[ot2])
                T.op("sp", lambda e, ob2=ob2, oc=oc, tsl=tsl: e.dma_start(out=y[oc * 128:(oc + 1) * 128, tsl], in_=ob2[:]),
                     reads=[ot2], dma_lane=f"do{s2}")
                C.out_tokens.append(ot2)
    return None


def w_layout(w, pad_to=None):
    din, dout = w.shape
    if pad_to is not None and pad_to > dout:
        w = np.concatenate([w, np.zeros((din, pad_to - dout), w.dtype)], axis=1)
        dout = pad_to
    kc, oc = din // 128, dout // 128
    return np.ascontiguousarray(w.reshape(kc, 128, oc, 128).transpose(2, 1, 0, 3).reshape(oc, 128, kc * 128))


def vec_layout(v):
    return np.ascontiguousarray(v.reshape(-1, 128).T)


def make_ident(C, n=128):
    one = C.sb("id_one", [n, n], F32)
    ident = C.sb("ident", [n, n], F32)
    C.T.op("dve", lambda e: e.memset(one[:], 1.0), writes=["id_one"])
    C.T.op("pool", lambda e: e.affine_select(out=ident[:], in_=one[:], pattern=[[-1, n]], compare_op=ALU.is_equal,
                                             fill=0.0, base=0, channel_multiplier=1),
           reads=["id_one"], writes=["ident"])
    return ident


def emit_xattn(C, NT=NTOK):
    T = C.T
    q = C.dram_in("q", [D_MODEL, NT])
    k = C.dram_in("k", [D_MODEL, N_MEM])
    vT = C.dram_in("vT", [D_MODEL, N_MEM])
    y = C.dram_out("y", [D_MODEL, NT])
    scale = 512.0 ** -0.5
    ident = make_ident(C)
    qT = C.sb("qT", [128, 16, NT], F32R)
    kT = C.sb("kT", [128, 16, N_MEM], F32R)
    vt = C.sb("vt", [128, 2, D_MODEL], F32R)
    qv = q.rearrange("(c p) t -> p c t", p=128)
    for c0 in range(0, 16, 4):
        T.op("pool", lambda e, c0=c0: e.dma_start(out=qT[:, c0:c0 + 4, :], in_=qv[:, c0:c0 + 4, :]),
             writes=[("qT", c) for c in range(c0, c0 + 4)], dma_lane="dq")
    T.op("pool", lambda e: e.dma_start(out=kT[:], in_=k.rearrange("(c p) m -> p c m", p=128)), writes=["kT"], dma_lane="dk")
    vTs = C.sb("vTs", [128, 16, N_MEM], F32)
    T.op("sp", lambda e: e.dma_start(out=vTs[:], in_=vT.rearrange("(c p) m -> p c m", p=128)), writes=["vTs"], dma_lane="dv")
    ps_v = Rot([C.ps(f"psv{i}") for i in range(2)], "psv")
    for c in range(16):
        for mc in range(2):
            _, pv, pvt = ps_v.next()
            T.op("pe", lambda e, pv=pv, c=c, mc=mc: e.transpose(out=pv[:, :128], in_=vTs[:, c, mc * 128:(mc + 1) * 128], identity=ident[:]),
                 reads=["vTs", "ident"], writes=[pvt])
            T.op("act", lambda e, pv=pv, c=c, mc=mc: e.copy(out=vt[:, mc, c * 128:(c + 1) * 128], in_=pv[:, :128]), reads=[pvt], writes=["vt"])
    p_sb = Rot([C.sb(f"p{i}", [128, N_MEM], F32) for i in range(2)], "p")
    pn_sb = Rot([C.sb(f"pn{i}", [128, N_MEM], F32) for i in range(2)], "pn")
    st_rot = Rot([C.sb(f"stt{i}", [128, 4], F32) for i in range(2)], "st")
    pT = Rot([C.sb(f"pT{i}", [128, 2, 512], F32R) for i in range(2)], "pT")
    ob = Rot([C.sb(f"ob{i}", [128, 512], F32) for i in range(3)], "ob")
    ps_s = Rot([C.ps(f"pss{i}") for i in range(2)], "pss")
    ps_t = Rot([C.ps(f"pst{i}") for i in range(2)], "pst")
    ps_o = Rot([C.ps(f"pso{i}") for i in range(2)], "pso")
    for tt in range(NT // 512):
        for h in range(XA_HEADS):
            _, pTb, pTt = pT.next()
            for sub in range(4):
                t0 = tt * 512 + sub * 128
                _, pss, psst = ps_s.next()
                for dc in range(4):
                    c = h * 4 + dc
                    T.op("pe", lambda e, c=c, t0=t0, pss=pss, dc=dc: e.matmul(pss[:, :N_MEM], lhsT=qT[:, c, t0:t0 + 128],
                                                                           rhs=kT[:, c, :], start=(dc == 0), stop=(dc == 3)),
                         reads=[("qT", c), "kT"], writes=[psst])
                _, stb, stt_ = st_rot.next()
                T.op("dve", lambda e, pss=pss, stb=stb: e.tensor_reduce(out=stb[:, 0:1], in_=pss[:, :N_MEM], axis=AX.X, op=ALU.max),
                     reads=[psst], writes=[stt_])
                T.op("dve", lambda e, stb=stb: e.tensor_scalar(out=stb[:, 1:2], in0=stb[:, 0:1], scalar1=-scale, scalar2=None,
                                                               op0=ALU.mult),
                     reads=[stt_], writes=[stt_])
                _, pb, pt = p_sb.next()
                T.op("act", lambda e, pb=pb, pss=pss, stb=stb: e.activation(out=pb[:], in_=pss[:, :N_MEM], func=AF.Exp,
                                                                          bias=stb[:, 1:2], scale=scale),
                     reads=[psst, stt_], writes=[pt])
                T.op("dve", lambda e, pb=pb, stb=stb: e.tensor_reduce(out=stb[:, 2:3], in_=pb[:], axis=AX.X, op=ALU.add),
                     reads=[pt], writes=[stt_])
                T.op("dve", lambda e, stb=stb: e.reciprocal(out=stb[:, 3:4], in_=stb[:, 2:3]), reads=[stt_], writes=[stt_])
                _, pnb, pnt = pn_sb.next()
                T.op("dve", lambda e, pb=pb, pnb=pnb, stb=stb: e.tensor_scalar(out=pnb[:], in0=pb[:], scalar1=stb[:, 3:4],
                                                                             scalar2=None, op0=ALU.mult),
                     reads=[pt, stt_], writes=[pnt])
                for mc in range(2):
                    _, pst, pstt = ps_t.next()
                    T.op("pe", lambda e, pst=pst, pnb=pnb, mc=mc: e.transpose(out=pst[:, :128], in_=pnb[:, mc * 128:(mc + 1) * 128],
                                                                           identity=ident[:]),
                         reads=[pnt, "ident"], writes=[pstt])
                    T.op("act", lambda e, pst=pst, pTb=pTb, mc=mc, sub=sub: e.copy(out=pTb[:, mc, sub * 128:(sub + 1) * 128],
                                                                                 in_=pst[:, :128]),
                         reads=[pstt], writes=[pTt])
            for dc in range(4):
                c = h * 4 + dc
                _, pso, psot = ps_o.next()
                for mc in range(2):
                    T.op("pe", lambda e, pso=pso, mc=mc, c=c, pTb=pTb: e.matmul(pso[:], lhsT=vt[:, mc, c * 128:(c + 1) * 128],
                                                                             rhs=pTb[:, mc, :], start=(mc == 0), stop=(mc == 1)),
                         reads=["vt", pTt], writes=[psot])
                s, obb, obt = ob.next()
                T.op("dve", lambda e, obb=obb, pso=pso: e.tensor_copy(out=obb[:], in_=pso[:]), reads=[psot], writes=[obt])
                T.op("sp", lambda e, obb=obb, c=c, tt=tt: e.dma_start(out=y[c * 128:(c + 1) * 128, tt * 512:(tt + 1) * 512], in_=obb[:]),
                     reads=[obt], dma_lane=f"do{s}")
                C.out_tokens.append(obt)
    return None


def emit_gmlp(C, NT=NTOK):
    T = C.T
    u = C.dram_in("u", [GM_WIDTH, NT])
    v = C.dram_in("v", [GM_WIDTH, NT])
    smwT = C.dram_in("smwT", [GM_HEADS, 128, 128])
    smb = C.dram_in("smb", [1, GM_HEADS * 128])
    lng = C.dram_in("lng", [128, 8])
    lnb = C.dram_in("lnb", [128, 8])
    y = C.dram_out("y", [GM_WIDTH, NT])
    ones = make_ones(C)
    ident = make_ident(C)
    H = GM_HEADS
    gu = C.sb("gu", [128, H, 512], F32)
    gv = C.sb("gv", [128, H, 512], F32)
    gvr = C.sb("gvr", [128, H, 512], F32R)
    lg = C.sb("lg", [128, 8], F32)
    lb = C.sb("lb", [128, 8], F32)
    wm32 = C.sb("wm32", [128, H, 128], F32)
    wmm = C.sb("wmm", [128, H, 128], F32)
    wm = C.sb("wm", [128, H, 128], F32R)
    bbc = C.sb("bbc", [128, H * 128], F32)
    T.op("sp", lambda e: e.dma_start(out=lg[:], in_=lng), writes=["lg"], dma_lane="d0")
    T.op("sp", lambda e: e.dma_start(out=lb[:], in_=lnb), writes=["lb"], dma_lane="d1")
    T.op("sp", lambda e: e.dma_start(out=wm32[:], in_=smwT.rearrange("h s t -> s h t")), writes=["wm32"], dma_lane="d2")
    T.op("sp", lambda e: e.dma_start(out=bbc[:], in_=smb.broadcast_to([128, H * 128])), writes=["bbc"], dma_lane="d3")
    for h in range(H):
        T.op("pool", lambda e, h=h: e.affine_select(out=wmm[:, h, :], in_=wm32[:, h, :], pattern=[[1, 128]],
                                                    compare_op=ALU.is_ge, fill=0.0, base=0, channel_multiplier=-1),
             reads=["wm32"], writes=[("wmm", h)])
    T.op("dve", lambda e: e.tensor_copy(out=wm[:], in_=wmm[:]), reads=[("wmm", h) for h in range(H)], writes=["wm"])
    uv = u.rearrange("(c p) t -> p c t", p=128)
    vv = v.rearrange("(c p) t -> p c t", p=128)
    psn = C.ps("psn")
    psm = C.ps("psm")
    mean = C.sb("mean", [128, 512], F32)
    cen = gv
    vn = gv
    rstd = C.sb("rstd", [128, 512], F32)
    tmp = C.sb("tmp", [128, 512], F32)
    sq_rot = Rot([C.sb(f"sq{i}", [128, 512], F32R) for i in range(2)], "sq")
    vtok = Rot([C.sb(f"vtok{i}", [128, 4, 128], F32R) for i in range(2)], "vtok")
    ps_t = Rot([C.ps(f"pst{i}") for i in range(2)], "pst")
    ps_x = Rot([C.ps(f"psx{i}") for i in range(2)], "psx")
    t1 = Rot([C.sb(f"t1{i}", [128, 512], F32) for i in range(2)], "t1")
    ob = Rot([C.sb(f"ob{i}", [128, 512], F32) for i in range(3)], "ob")
    for tt in range(NT // 512):
        tsl = slice(tt * 512, (tt + 1) * 512)
        for c0 in range(0, H, 4):
            T.op("sp", lambda e, c0=c0, tsl=tsl: e.dma_start(out=gu[:, c0:c0 + 4, :], in_=uv[:, c0:c0 + 4, tsl]),
                 writes=[("gu", c) for c in range(c0, c0 + 4)], dma_lane="d4")
            T.op("sp", lambda e, c0=c0, tsl=tsl: e.dma_start(out=gv[:, c0:c0 + 4, :], in_=vv[:, c0:c0 + 4, tsl]),
                 writes=[("gv", c) for c in range(c0, c0 + 4)], dma_lane="d5")
        for c in range(H):
            T.op("act", lambda e, c=c: e.activation(out=gu[:, c, :], in_=gu[:, c, :], func=AF.Gelu), reads=[("gu", c)], writes=[("gu", c)])
            T.op("act", lambda e, c=c: e.activation(out=gv[:, c, :], in_=gv[:, c, :], func=AF.Gelu), reads=[("gv", c)], writes=[("gv", c)])
        for c in range(H):
            T.op("dve", lambda e, c=c: e.tensor_copy(out=gvr[:, c, :], in_=gv[:, c, :]), reads=[("gv", c)], writes=[("gvr", c)])
            T.op("pe", lambda e, c=c: e.matmul(psm[:], lhsT=ones[:], rhs=gvr[:, c, :], start=(c == 0), stop=(c == H - 1)),
                 reads=[("gvr", c), "ones"], writes=["psm"])
        T.op("dve", lambda e: e.tensor_scalar(out=mean[:], in0=psm[:], scalar1=1.0 / GM_WIDTH, scalar2=None, op0=ALU.mult),
             reads=["psm"], writes=["mean"])
        for c in range(H):
            T.op("dve", lambda e, c=c: e.tensor_tensor(out=cen[:, c, :], in0=gv[:, c, :], in1=mean[:], op=ALU.subtract),
                 reads=[("gv", c), "mean"], writes=[("gv", c)])
        emit_rstd(C, ones, lambda c: cen[:, c, :], [("gv", c) for c in range(H)], H, 512, GM_WIDTH, psn, sq_rot, rstd, tmp, "ln")
        for c in range(H):
            T.op("dve", lambda e, c=c: e.scalar_tensor_tensor(out=cen[:, c, :], in0=cen[:, c, :], scalar=lg[:, c:c + 1], in1=rstd[:],
                                                              op0=ALU.mult, op1=ALU.mult),
                 reads=[("gv", c), "lg", ("rstd", "ln")], writes=[("gv", c)])
            T.op("act", lambda e, c=c: e.activation(out=vn[:, c, :], in_=cen[:, c, :], func=AF.Identity, bias=lb[:, c:c + 1], scale=1.0),
                 reads=[("gv", c), "lb"], writes=[("gv", c)])
        for c in range(H):
            _, vtb, vtt = vtok.next()
            for n in range(4):
                _, pst, pstt = ps_t.next()
                T.op("pe", lambda e, pst=pst, c=c, n=n: e.transpose(out=pst[:, :128], in_=vn[:, c, n * 128:(n + 1) * 128], identity=ident[:]),
                     reads=[("gv", c), "ident"], writes=[pstt])
                T.op("act", lambda e, pst=pst, vtb=vtb, n=n: e.copy(out=vtb[:, n, :], in_=pst[:, :128]), reads=[pstt], writes=[vtt])
            _, psx, psxt = ps_x.next()
            for n in range(4):
                T.op("pe", lambda e, psx=psx, vtb=vtb, c=c, n=n: e.matmul(psx[:, n * 128:(n + 1) * 128], lhsT=vtb[:, n, :], rhs=wm[:, c, :],
                                                                       start=True, stop=True),
                     reads=[vtt, "wm"], writes=[psxt])
            _, t1b, t1t = t1.next()
            for n in range(4):
                T.op("dve", lambda e, psx=psx, t1b=t1b, c=c, n=n: e.tensor_tensor(out=t1b[:, n * 128:(n + 1) * 128],
                                                                               in0=psx[:, n * 128:(n + 1) * 128],
                                                                               in1=bbc[:, c * 128:(c + 1) * 128], op=ALU.add),
                     reads=[psxt, "bbc"], writes=[t1t])
            s, obb, obt = ob.next()
            T.op("dve", lambda e, obb=obb, t1b=t1b, c=c, tsl=tsl: e.tensor_tensor(out=obb[:], in0=t1b[:], in1=gu[:, c, :], op=ALU.mult),
                 reads=[t1t, ("gu", c)], writes=[obt])
            T.op("sp", lambda e, obb=obb, c=c, tsl=tsl: e.dma_start(out=y[c * 128:(c + 1) * 128, tsl], in_=obb[:]), reads=[obt], dma_lane=f"do{s}")
            C.out_tokens.append(obt)
    return None


def emit_dna(C, NP=4, TS=SEQ):
    T = C.T
    qkv = C.dram_in("qkv", [NP, 3, 128, TS])
    cw = C.dram_in("cw", [NP, 3, 128, 4])
    br = C.dram_in("br", [NP, 1, TS])
    ar = C.dram_in("ar", [NP, 1, TS])
    al = C.dram_in("al", [NP, 1, 1])
    db = C.dram_in("db", [NP, 1, 1])
    oq = C.dram_out("oq", [NP, 128, TS])
    ok = C.dram_out("ok", [NP, 128, TS])
    ov = C.dram_out("ov", [NP, 128, TS])
    ob = C.dram_out("ob", [NP, 1, TS])
    og = C.dram_out("og", [NP, 1, TS])
    ones = make_ones(C)
    xp = Rot([C.sb(f"xp{i}", [128, TS + 3], F32) for i in range(2)], "xp")
    acc = Rot([C.sb(f"acc{i}", [128, TS], F32) for i in range(2)], "acc")
    sil = Rot([C.sb(f"sil{i}", [128, TS], F32) for i in range(2)], "sil")
    outb = Rot([C.sb(f"outb{i}", [128, TS], F32) for i in range(2)], "outb")
    cwt = Rot([C.sb(f"cwt{i}", [128, 4], F32) for i in range(2)], "cwt")
    sq = Rot([C.sb(f"sq{i}", [128, 512], F32R) for i in range(2)], "sq")
    psn = Rot([C.ps(f"psn{i}") for i in range(2)], "psn")
    tmp = Rot([C.sb(f"tmp{i}", [128, 512], F32) for i in range(2)], "tmp")
    rs = Rot([C.sb(f"rs{i}", [128, 512], F32) for i in range(2)], "rs")
    row = Rot([C.sb(f"row{i}", [1, TS], F32) for i in range(4)], "row")
    sc = Rot([C.sb(f"sc{i}", [1, 4], F32) for i in range(2)], "sc")
    for i in range(2):
        T.op("dve", lambda e, i=i: e.memset(xp.bufs[i][:, 0:3], 0.0), writes=[("xp", i)])
    for p in range(NP):
        for j in range(3):
            s, xb, xt = xp.next()
            T.op("sp", lambda e, xb=xb, p=p, j=j: e.dma_start(out=xb[:, 3:], in_=qkv[p, j]), writes=[xt], dma_lane=f"dx{s}")
            s2, cb, ct = cwt.next()
            T.op("sp", lambda e, cb=cb, p=p, j=j: e.dma_start(out=cb[:], in_=cw[p, j]), writes=[ct], dma_lane=f"dc{s2}")
            _, ab, at = acc.next()
            T.op("dve", lambda e, ab=ab, xb=xb, cb=cb: e.tensor_scalar(out=ab[:], in0=xb[:, 0:TS], scalar1=cb[:, 0:1], scalar2=None,
                                                                     op0=ALU.mult),
                 reads=[xt, ct], writes=[at])
            for i in range(1, 4):
                T.op("dve", lambda e, ab=ab, xb=xb, cb=cb, i=i: e.scalar_tensor_tensor(out=ab[:], in0=xb[:, i:i + TS], scalar=cb[:, i:i + 1],
                                                                                     in1=ab[:], op0=ALU.mult, op1=ALU.add),
                     reads=[xt, ct, at], writes=[at])
            _, sb_, st_ = sil.next()
            T.op("act", lambda e, sb_=sb_, ab=ab: e.activation(out=sb_[:], in_=ab[:], func=AF.Silu), reads=[at], writes=[st_])
            dst = (oq, ok, ov)[j]
            if j == 2:
                T.op("sp", lambda e, sb_=sb_, p=p: e.dma_start(out=ov[p], in_=sb_[:]), reads=[st_], dma_lane="dov")
                C.out_tokens.append(st_)
                continue
            so, obf, obt = outb.next()
            for wt in range(TS // 512):
                wsl = slice(wt * 512, (wt + 1) * 512)
                _, sqb, sqt = sq.next()
                T.op("act", lambda e, sqb=sqb, sb_=sb_, wsl=wsl: e.activation(out=sqb[:], in_=sb_[:, wsl], func=AF.Square),
                     reads=[st_], writes=[sqt])
                _, pn, pnt = psn.next()
                T.op("pe", lambda e, pn=pn, sqb=sqb: e.matmul(pn[:], lhsT=ones[:], rhs=sqb[:], start=True, stop=True),
                     reads=[sqt, "ones"], writes=[pnt])
                _, tb, tt_ = tmp.next()
                T.op("dve", lambda e, tb=tb, pn=pn: e.tensor_scalar(out=tb[:], in0=pn[:], scalar1=1.0, scalar2=EPS, op0=ALU.mult, op1=ALU.add),
                     reads=[pnt], writes=[tt_])
                T.op("act", lambda e, tb=tb: e.activation(out=tb[:], in_=tb[:], func=AF.Sqrt), reads=[tt_], writes=[tt_])
                _, rb, rt = rs.next()
                T.op("dve", lambda e, rb=rb, tb=tb: e.reciprocal(out=rb[:], in_=tb[:]), reads=[tt_], writes=[rt])
                qs = (128.0 ** -0.5) if j == 0 else 1.0
                T.op("dve", lambda e, obf=obf, sb_=sb_, rb=rb, wsl=wsl, qs=qs: e.scalar_tensor_tensor(out=obf[:, wsl], in0=sb_[:, wsl], scalar=qs,
                                                                                                   in1=rb[:], op0=ALU.mult, op1=ALU.mult),
                     reads=[st_, rt], writes=[obt])
            T.op("sp", lambda e, obf=obf, p=p, dst=dst: e.dma_start(out=dst[p], in_=obf[:]), reads=[obt], dma_lane=f"doo{so}")
            C.out_tokens.append(obt)
        s, r_b, r_bt = row.next()
        T.op("sp", lambda e, r_b=r_b, p=p: e.dma_start(out=r_b[:], in_=br[p]), writes=[r_bt], dma_lane=f"dr{s}")
        T.op("act", lambda e, r_b=r_b: e.activation(out=r_b[:], in_=r_b[:], func=AF.Sigmoid), reads=[r_bt], writes=[r_bt])
        T.op("sp", lambda e, r_b=r_b, p=p: e.dma_start(out=ob[p], in_=r_b[:]), reads=[r_bt], dma_lane=f"dro{s}")
        C.out_tokens.append(r_bt)
        s, r_a, r_at = row.next()
        T.op("sp", lambda e, r_a=r_a, p=p: e.dma_start(out=r_a[:], in_=ar[p]), writes=[r_at], dma_lane=f"dr{s}")
        s3, scb, sct = sc.next()
        T.op("sp", lambda e, scb=scb, p=p: e.dma_start(out=scb[:, 0:1], in_=al[p]), writes=[sct], dma_lane=f"ds{s3}a")
        T.op("sp", lambda e, scb=scb, p=p: e.dma_start(out=scb[:, 1:2], in_=db[p]), writes=[sct], dma_lane=f"ds{s3}b")
        T.op("act", lambda e, r_a=r_a, scb=scb: e.activation(out=r_a[:], in_=r_a[:], func=AF.Exp, bias=scb[:, 1:2], scale=1.0),
             reads=[r_at, sct], writes=[r_at])
        T.op("dve", lambda e, r_a=r_a: e.tensor_scalar(out=r_a[:], in0=r_a[:], scalar1=1.0, scalar2=None, op0=ALU.add),
             reads=[r_at], writes=[r_at])
        T.op("act", lambda e, r_a=r_a: e.activation(out=r_a[:], in_=r_a[:], func=AF.Ln), reads=[r_at], writes=[r_at])
        T.op("act", lambda e, scb=scb: e.activation(out=scb[:, 2:3], in_=scb[:, 0:1], func=AF.Exp), reads=[sct], writes=[sct])
        T.op("dve", lambda e, r_a=r_a, scb=scb: e.tensor_scalar(out=r_a[:], in0=r_a[:], scalar1=scb[:, 2:3], scalar2=-1.0,
                                                              op0=ALU.mult, op1=ALU.mult),
             reads=[r_at, sct], writes=[r_at])
        T.op("sp", lambda e, r_a=r_a, p=p: e.dma_start(out=og[p], in_=r_a[:]), reads=[r_at], dma_lane=f"dro{s}")
        C.out_tokens.append(r_at)
    return None


def emit_dnb(C, NP=4, TS=SEQ, ondev=False):
    T = C.T
    CH = 64
    NCH = TS // CH
    qT_d = C.dram_in("qT", [NP, 128, TS])
    kT_d = C.dram_in("kT", [NP, 128, TS])
    kt_d = C.dram_in("ktok", [NP, CH, NCH, 128])
    vt_d = C.dram_in("vtok", [NP, CH, NCH, 128])
    gc_d = C.dram_in("gcol", [NP, CH, NCH])
    bc_d = C.dram_in("bcol", [NP, CH, NCH])
    br_d = C.dram_in("brow", [NP, 1, TS])
    z_d = C.dram_in("zT", [NP, 128, TS])
    nw_d = C.dram_in("nw", [128, 1])
    y = C.dram_out("y", [NP, 128, TS])

    def sbt(name, shape):
        return C.sb(name, shape, F32)

    ident = make_ident(C)
    ones = sbt("ones", [128, 128])
    T.op("dve", lambda e: e.memset(ones[:], 1.0), writes=["ones"])
    U = sbt("U", [CH, CH])
    Us = sbt("Us", [CH, CH])
    T.op("pool", lambda e: e.affine_select(out=U[:], in_=ones[:CH, :CH], pattern=[[1, CH]], compare_op=ALU.is_ge, fill=0.0,
                                           base=0, channel_multiplier=-1), reads=["ones"], writes=["U"])
    T.op("pool", lambda e: e.affine_select(out=Us[:], in_=ones[:CH, :CH], pattern=[[1, CH]], compare_op=ALU.is_gt, fill=0.0,
                                           base=0, channel_multiplier=-1), reads=["ones"], writes=["Us"])
    def b3(ap2):
        return ap2.unsqueeze(1).broadcast_to([CH, NCH, CH])

    def c3(ap2, w):
        return ap2.unsqueeze(2).broadcast_to([ap2.shape[0], NCH, w])

    def v3(t, w=CH):
        return t.rearrange("p (c t) -> p c t", t=w)

    nw = sbt("nw", [128, 1])
    T.op("sp", lambda e: e.dma_start(out=nw[:], in_=nw_d), writes=["nw"], dma_lane="dnw")

    psA = C.ps("psA", (128, 2048))
    psB = C.ps("psB", (128, 2048))
    A_all = [("psA", b) for b in range(4)]
    B_all = [("psB", b) for b in range(4)]

    qT = sbt("qT", [128, TS]); kT = sbt("kT", [128, TS])
    ktok = sbt("ktok", [CH, NCH, 128]); vtok = sbt("vtok", [CH, NCH, 128])
    gcol = sbt("gcol", [CH, NCH]); bcol = sbt("bcol", [CH, NCH]); brow = sbt("brow", [1, TS])
    gcc = sbt("gcc", [CH, NCH]); egc = sbt("egc", [CH, NCH]); kds = sbt("kds", [CH, NCH]); egl = sbt("egl", [128, NCH])
    bg = sbt("bg", [CH, NCH])
    Gbc = sbt("Gbc", [CH, NCH, 128])
    decT = sbt("decT", [CH, TS])
    Mb = [sbt(f"M{i}", [CH, TS]) for i in range(2)]
    MTb = [sbt(f"MT{i}", [CH, TS]) for i in range(2)]
    RT = sbt("RT", [CH, TS])
    dtmp = Mb[1]; decTsb = MTb[1]; GU = v3(Mb[0][:]); qkT = Mb[0]
    kbg = sbt("kbg", [CH, NCH, 128]); vb = kbg; kdec = sbt("kdec", [CH, NCH, 128])
    wT = sbt("wT", [128, TS]); u_all = sbt("u_all", [CH, NCH, 128])
    qgT = sbt("qgT", [128, TS]); oT = sbt("oT", [128, TS]); zT = wT
    S = [sbt(f"S{i}", [128, 128]) for i in range(2)]
    vnew = [sbt(f"vnew{i}", [CH, 128]) for i in range(2)]
    sqf = Rot([sbt(f"sqf{i}", [128, 512]) for i in range(1)], "sqf")
    tmpf = Rot([sbt(f"tmpf{i}", [128, 512]) for i in range(1)], "tmpf")
    rsf = Rot([sbt(f"rsf{i}", [128, 512]) for i in range(1)], "rsf")
    yb = Rot([sbt(f"yb{i}", [128, 512]) for i in range(1)], "yb")

    def csl(c):
        return slice(c * CH, (c + 1) * CH)

    def bank(c):
        return c // 8

    for p in range(NP):
        if not ondev:
            for name, dst, src in (("qT", qT, qT_d), ("kT", kT, kT_d), ("ktok", ktok, kt_d), ("vtok", vtok, vt_d),
                                   ("gcol", gcol, gc_d), ("bcol", bcol, bc_d), ("brow", brow, br_d)):
                T.op("sp", lambda e, dst=dst, src=src, p=p: e.dma_start(out=dst[:], in_=src[p]), writes=[name], dma_lane="d_" + name)
        else:
            for name, dst, src in (("qT", qT, qT_d), ("kT", kT, kT_d), ("brow", brow, br_d), ("wT", wT, vt_d)):
                T.op("sp", lambda e, dst=dst, src=src, p=p: e.dma_start(out=dst[:], in_=src[p]), writes=[name], dma_lane="d_" + name)
            for name, dst, src in (("gcol", gcol, gc_d), ("bcol", bcol, bc_d)):
                T.op("sp", lambda e, dst=dst, src=src, p=p: e.dma_start(out=dst[:], in_=src[p].rearrange("o (c t) -> (o t) c", t=CH),
                                                                       allow_slow_non_contiguous=True),
                     writes=[name], dma_lane="d_" + name)
            for srcT, srctok, dstt, dname in ((kT, "kT", ktok, "ktok"), (wT, "wT", vtok, "vtok")):
                for half, P_, pn in ((0, psA, "psA"), (1, psB, "psB")):
                    for cc in range(16):
                        c = half * 16 + cc
                        T.op("pe", lambda e, c=c, cc=cc, P_=P_, srcT=srcT: e.transpose(out=P_[:CH, cc * 128:(cc + 1) * 128], in_=srcT[:, csl(c)], identity=ident[:]),
                             reads=[srctok, "ident"], writes=[(pn, cc // 4)])
                    T.op("act", lambda e, half=half, P_=P_, dstt=dstt: e.copy(out=dstt[:, half * 16:(half + 1) * 16, :].rearrange("p c d -> p (c d)"), in_=P_[:CH, :]),
                         reads=[(pn, b_) for b_ in range(4)], writes=[dname])
        T.op("pe", lambda e: e.matmul(psA[:CH, 0:NCH], lhsT=U[:], rhs=gcol[:], start=True, stop=True), reads=["U", "gcol"], writes=[("psA", 0)])
        T.op("pe", lambda e: e.matmul(psA[:, 512:512 + NCH], lhsT=ones[:CH, :], rhs=gcol[:], start=True, stop=True),
             reads=["ones", "gcol"], writes=[("psA", 1)])
        T.op("dve", lambda e: e.tensor_copy(out=gcc[:], in_=psA[:CH, 0:NCH]), reads=[("psA", 0)], writes=["gcc"])
        T.op("act", lambda e: e.activation(out=egc[:], in_=gcc[:], func=AF.Exp), reads=["gcc"], writes=["egc"])
        T.op("dve", lambda e: e.tensor_tensor(out=kds[:], in0=psA[:CH, 512:512 + NCH], in1=gcc[:], op=ALU.subtract),
             reads=[("psA", 1), "gcc"], writes=["kds"])
        T.op("act", lambda e: e.activation(out=kds[:], in_=kds[:], func=AF.Exp), reads=["kds"], writes=["kds"])
        T.op("act", lambda e: e.activation(out=egl[:], in_=psA[:, 512:512 + NCH], func=AF.Exp), reads=[("psA", 1)], writes=["egl"])
        T.op("dve", lambda e: e.tensor_tensor(out=bg[:], in0=bcol[:], in1=egc[:], op=ALU.mult), reads=["bcol", "egc"], writes=["bg"])
        T.op("dve", lambda e: e.tensor_tensor(out=Gbc[:], in0=ones[:CH, :].unsqueeze(1).broadcast_to([CH, NCH, 128]), in1=c3(gcol[:], 128), op=ALU.mult),
             reads=["ones", "gcol"], writes=["Gbc"])
        T.op("dve", lambda e: e.scalar_tensor_tensor(out=GU, in0=b3(U[:]), scalar=-1.0, in1=c3(gcol[:], CH), op0=ALU.mult, op1=ALU.mult),
             reads=["U", "gcol"], writes=[("M", 0)])
        for c in range(NCH):
            T.op("pe", lambda e, c=c: e.matmul(psA[:CH, csl(c)], lhsT=Gbc[:, c, 0:CH], rhs=U[:], start=True, stop=False),
                 reads=["Gbc", "U"], writes=[("psA", bank(c))])
            T.op("pe", lambda e, c=c: e.matmul(psA[:CH, csl(c)], lhsT=GU[:, c, :], rhs=ones[:CH, :CH], start=False, stop=True),
                 reads=[("M", 0), "ones"], writes=[("psA", bank(c))])
        T.op("dve", lambda e: e.tensor_scalar(out=dtmp[:], in0=psA[:CH, :], scalar1=0.0, scalar2=None, op0=ALU.min), reads=A_all, writes=[("M", 1)])
        T.op("act", lambda e: e.activation(out=dtmp[:], in_=dtmp[:], func=AF.Exp), reads=[("M", 1)], writes=[("M", 1)])
        T.op("dve", lambda e: e.tensor_tensor(out=v3(decT[:]), in0=v3(dtmp[:]), in1=b3(U[:]), op=ALU.mult),
             reads=[("M", 1), "U"], writes=["decT"])
        for w in range(4):
            T.op("pe", lambda e, w=w: e.matmul(psB[:CH, w * 512:(w + 1) * 512], lhsT=ones[0:1, 0:CH], rhs=brow[:, w * 512:(w + 1) * 512],
                                               start=True, stop=True), reads=["ones", "brow"], writes=[("psB", w)])
        T.op("dve", lambda e: e.tensor_tensor(out=v3(decTsb[:]), in0=v3(decT[:]), in1=b3(Us[:]), op=ALU.mult),
             reads=["decT", "Us"], writes=[("MT", 1)])
        T.op("dve", lambda e: e.tensor_tensor(out=decTsb[:], in0=decTsb[:], in1=psB[:CH, :], op=ALU.mult), reads=[("MT", 1)] + B_all, writes=[("MT", 1)])
        for c in range(NCH):
            T.op("pe", lambda e, c=c: e.matmul(psA[:CH, csl(c)], lhsT=kT[:, csl(c)], rhs=kT[:, csl(c)], start=True, stop=True),
                 reads=["kT"], writes=[("psA", bank(c))])
        T.op("dve", lambda e: e.scalar_tensor_tensor(out=MTb[0][:], in0=psA[:CH, :], scalar=-1.0, in1=decTsb[:], op0=ALU.mult, op1=ALU.mult),
             reads=A_all + [("MT", 1)], writes=[("MT", 0)])
        for c in range(NCH):
            T.op("pe", lambda e, c=c: e.transpose(out=psB[:CH, csl(c)], in_=MTb[0][:, csl(c)], identity=ident[:CH, :CH]),
                 reads=[("MT", 0), "ident"], writes=[("psB", bank(c))])
        T.op("act", lambda e: e.copy(out=Mb[0][:], in_=psB[:CH, :]), reads=B_all, writes=[("M", 0)])
        T.op("dve", lambda e: e.tensor_tensor(out=v3(RT[:]), in0=v3(MTb[0][:]), in1=b3(ident[:CH, :CH]), op=ALU.add),
             reads=[("MT", 0), "ident"], writes=["RT"])
        cur = 0
        for lvl in range(5):
            nxt = 1 - cur
            for c in range(NCH):
                T.op("pe", lambda e, c=c, cur=cur: e.matmul(psA[:CH, csl(c)], lhsT=MTb[cur][:, csl(c)], rhs=Mb[cur][:, csl(c)], start=True, stop=True),
                     reads=[("MT", cur), ("M", cur)], writes=[("psA", bank(c))])
            if lvl < 4:
                for c in range(NCH):
                    T.op("pe", lambda e, c=c, cur=cur: e.matmul(psB[:CH, csl(c)], lhsT=Mb[cur][:, csl(c)], rhs=MTb[cur][:, csl(c)], start=True, stop=True),
                         reads=[("MT", cur), ("M", cur)], writes=[("psB", bank(c))])
            T.op("act", lambda e, nxt=nxt: e.copy(out=Mb[nxt][:], in_=psA[:CH, :]), reads=A_all, writes=[("M", nxt)])
            if lvl < 4:
                T.op("dve", lambda e, nxt=nxt: e.tensor_copy(out=MTb[nxt][:], in_=psB[:CH, :]), reads=B_all, writes=[("MT", nxt)])
            for c in range(NCH):
                T.op("pe", lambda e, c=c, nxt=nxt: e.matmul(psA[:CH, csl(c)], lhsT=Mb[nxt][:, csl(c)], rhs=RT[:, csl(c)], start=True, stop=True),
                     reads=[("M", nxt), "RT"], writes=[("psA", bank(c))])
            T.op("dve", lambda e: e.tensor_tensor(out=RT[:], in0=RT[:], in1=psA[:CH, :], op=ALU.add), reads=["RT"] + A_all, writes=["RT"])
            cur = nxt
        T.op("dve", lambda e: e.tensor_tensor(out=kbg[:], in0=ktok[:], in1=c3(bg[:], 128), op=ALU.mult), reads=["ktok", "bg"], writes=["kbg"])
        T.op("pool", lambda e: e.tensor_tensor(out=kdec[:], in0=ktok[:], in1=c3(kds[:], 128), op=ALU.mult), reads=["ktok", "kds"], writes=["kdec"])
        for c in range(NCH):
            T.op("pe", lambda e, c=c: e.matmul(psA[:, csl(c)], lhsT=kbg[:, c, :], rhs=RT[:, csl(c)], start=True, stop=True),
                 reads=["kbg", "RT"], writes=[("psA", bank(c))])
        T.op("act", lambda e: e.copy(out=wT[:], in_=psA[:, :]), reads=A_all, writes=["wT"])
        T.op("dve", lambda e: e.tensor_tensor(out=vb[:], in0=vtok[:], in1=c3(bcol[:], 128), op=ALU.mult), reads=["vtok", "bcol"], writes=["kbg"])
        for half in range(2):
            for cc in range(16):
                c = half * 16 + cc
                T.op("pe", lambda e, c=c, cc=cc: e.matmul(psB[:CH, cc * 128:(cc + 1) * 128], lhsT=RT[:, csl(c)], rhs=vb[:, c, :], start=True, stop=True),
                     reads=["RT", "kbg"], writes=[("psB", cc // 4)])
            T.op("dve", lambda e, half=half: e.tensor_copy(out=u_all[:, half * 16:(half + 1) * 16, :].rearrange("p c d -> p (c d)"), in_=psB[:CH, :]),
                 reads=B_all, writes=["u_all"])
        for c in range(NCH):
            T.op("pe", lambda e, c=c: e.matmul(psA[:CH, csl(c)], lhsT=kT[:, csl(c)], rhs=qT[:, csl(c)], start=True, stop=True),
                 reads=["kT", "qT"], writes=[("psA", bank(c))])
        T.op("dve", lambda e: e.tensor_tensor(out=qkT[:], in0=psA[:CH, :], in1=decT[:], op=ALU.mult), reads=A_all + ["decT"], writes=[("M", 0)])
        for c in range(NCH):
            T.op("pe", lambda e, c=c: e.matmul(psB[:, csl(c)], lhsT=Gbc[:, c, :], rhs=U[:], start=True, stop=True),
                 reads=["Gbc", "U"], writes=[("psB", bank(c))])
        T.op("act", lambda e: e.activation(out=qgT[:], in_=psB[:, :], func=AF.Exp), reads=B_all, writes=["qgT"])
        T.op("dve", lambda e: e.tensor_tensor(out=qgT[:], in0=qT[:], in1=qgT[:], op=ALU.mult), reads=["qT", "qgT"], writes=["qgT"])
        T.op("dve", lambda e: e.memset(S[0][:], 0.0), writes=[("S", 0)])
        for c in range(NCH):
            a = c % 2
            P_ = psA if a == 0 else psB
            pn = "psA" if a == 0 else "psB"
            Sc, Sn = S[a], S[1 - a]
            T.op("pe", lambda e, c=c, P_=P_, Sc=Sc: e.matmul(P_[:CH, 0:128], lhsT=wT[:, csl(c)], rhs=Sc[:], start=True, stop=True),
                 reads=["wT", ("S", a)], writes=[(pn, 0)])
            T.op("dve", lambda e, c=c, P_=P_, a=a: e.tensor_tensor(out=vnew[a][:], in0=u_all[:, c, :], in1=P_[:CH, 0:128], op=ALU.subtract),
                 reads=["u_all", (pn, 0)], writes=[("vnew", a)])
            T.op("pe", lambda e, c=c, P_=P_, Sc=Sc: e.matmul(P_[:, 512:512 + CH], lhsT=Sc[:], rhs=qgT[:, csl(c)], start=True, stop=False),
                 reads=["qgT", ("S", a)], writes=[(pn, 1)])
            T.op("pe", lambda e, c=c, P_=P_, a=a: e.matmul(P_[:, 512:512 + CH], lhsT=vnew[a][:], rhs=qkT[:, csl(c)], start=False, stop=True),
                 reads=[("M", 0), ("vnew", a)], writes=[(pn, 1)])
            T.op("act", lambda e, c=c, P_=P_: e.copy(out=oT[:, csl(c)], in_=P_[:, 512:512 + CH]), reads=[(pn, 1)], writes=["oT"])
            T.op("pe", lambda e, c=c, P_=P_, a=a: e.matmul(P_[:, 1024:1024 + 128], lhsT=kdec[:, c, :], rhs=vnew[a][:], start=True, stop=True),
                 reads=["kdec", ("vnew", a)], writes=[(pn, 2)])
            T.op("dve", lambda e, c=c, P_=P_, Sc=Sc, Sn=Sn: e.scalar_tensor_tensor(out=Sn[:], in0=Sc[:], scalar=egl[:, c:c + 1], in1=P_[:, 1024:1024 + 128],
                                                                                  op0=ALU.mult, op1=ALU.add),
                 reads=[("S", a), "egl", (pn, 2)], writes=[("S", 1 - a)])
        T.op("sp", lambda e, p=p: e.dma_start(out=zT[:], in_=z_d[p]), writes=["wT"], dma_lane="d_zT")
        for w in range(TS // 512):
            wsl = slice(w * 512, (w + 1) * 512)
            _, sqb, sqt = sqf.next()
            T.op("act", lambda e, sqb=sqb, wsl=wsl: e.activation(out=sqb[:], in_=oT[:, wsl], func=AF.Square), reads=["oT"], writes=[sqt])
            T.op("pe", lambda e, sqb=sqb, w=w: e.matmul(psA[:, 1536:2048], lhsT=ones[:], rhs=sqb[:], start=True, stop=True),
                 reads=[sqt, "ones"], writes=[("psA", 3)])
            _, tb, tt_ = tmpf.next()
            T.op("dve", lambda e, tb=tb: e.tensor_scalar(out=tb[:], in0=psA[:, 1536:2048], scalar1=1.0 / 128, scalar2=EPS, op0=ALU.mult, op1=ALU.add),
                 reads=[("psA", 3)], writes=[tt_])
            T.op("act", lambda e, tb=tb: e.activation(out=tb[:], in_=tb[:], func=AF.Sqrt), reads=[tt_], writes=[tt_])
            _, rb, rt = rsf.next()
            T.op("dve", lambda e, rb=rb, tb=tb: e.reciprocal(out=rb[:], in_=tb[:]), reads=[tt_], writes=[rt])
            T.op("dve", lambda e, rb=rb, wsl=wsl: e.scalar_tensor_tensor(out=rb[:], in0=oT[:, wsl], scalar=nw[:, 0:1], in1=rb[:], op0=ALU.mult, op1=ALU.mult),
                 reads=["oT", "nw", rt], writes=[rt])
            T.op("act", lambda e, tb=tb, wsl=wsl: e.activation(out=tb[:], in_=zT[:, wsl], func=AF.Silu), reads=["wT", tt_], writes=[tt_])
            s, ybb, ybt = yb.next()
            T.op("dve", lambda e, ybb=ybb, rb=rb, tb=tb: e.tensor_tensor(out=ybb[:], in0=rb[:], in1=tb[:], op=ALU.mult), reads=[rt, tt_], writes=[ybt])
            T.op("sp", lambda e, ybb=ybb, p=p, wsl=wsl: e.dma_start(out=y[p, :, wsl], in_=ybb[:]), reads=[ybt], dma_lane=f"dy{s}")
            C.out_tokens.append(ybt)
    return None


class RowSplit:
    def __init__(self, a, na, b):
        self.a, self.na, self.b = a, na, b

    def __getitem__(self, key):
        rs, cs = key
        if rs.start < self.na:
            return self.a[rs, cs]
        return self.b[slice(rs.start - self.na, rs.stop - self.na), cs]


PROJ_ROWS = 6272
PDN_ROWS = PROJ_ROWS - 2048


def _ext_in(C, name, shape):
    return C.nc.dram_tensor(name, list(shape), F32, kind="ExternalInput").ap()


def _ext_out(C, name, shape):
    return C.nc.dram_tensor(name, list(shape), F32, kind="ExternalOutput").ap()


_PROGS = {}


def run_prog(key, builder, in_maps):
    if key not in _PROGS:
        _PROGS[key] = builder()
    res = run_bass_kernel_spmd(_PROGS[key], in_maps, core_ids=list(range(len(in_maps))))
    return res.results


NUSE = 2
NB = BATCH // NUSE
G = 2 * NB
NPAIR = NB * DN_HEADS
NTC = G * NTOK


class _Sel:
    def __init__(self, fn):
        self.fn = fn

    def __getitem__(self, key):
        return self.fn(key)


LIN_W = {
    "f1a": ("ffn1_w_gate_up", 2 * D_FF, D_MODEL), "f1b": ("ffn1_w_down", D_MODEL, D_FF),
    "ip": ("w_in", PROJ_ROWS, D_MODEL), "wo": ("w_out", D_MODEL, D_MODEL), "xq": ("w_xq", D_MODEL, D_MODEL),
    "kv": ("w_xkv", 2 * D_MODEL, D_MODEL), "xo": ("w_xo", D_MODEL, D_MODEL),
    "f2a": ("ffn2_w_gate_up", 2 * D_FF, D_MODEL), "f2b": ("ffn2_w_down", D_MODEL, D_FF),
}
VECS = {
    "f1a_g": ("ffn1_norm_pre", D_MODEL), "f1b_pg": ("ffn1_norm_post", D_MODEL), "ip_g": ("mix_norm_pre", D_MODEL),
    "wo_pg": ("mix_norm_post", D_MODEL), "xq_g": ("xa_norm_pre", D_MODEL), "kv_g": ("mem_norm", D_MODEL),
    "xo_pg": ("xa_norm_post", D_MODEL), "f2a_g": ("ffn2_norm_pre", D_MODEL), "f2b_pg": ("ffn2_norm_post", D_MODEL),
    "gm_lng": ("sm_ln_g", GM_WIDTH), "gm_lnb": ("sm_ln_b", GM_WIDTH),
}


def prog_full():
    C = Ctx()
    E = lambda name, shape: _ext_in(C, name, shape)
    x_all = E("x_all", [D_MODEL, NTC])
    memT = E("memT", [NB, D_MODEL, N_MEM])
    out_all = _ext_out(C, "out_all", [D_MODEL, NTC])
    W = {}
    for l in range(DEPTH):
        for st_, (key, rows, cols) in LIN_W.items():
            W[(l, st_ + "_w")] = E(f"l{l}_{st_}_w", [rows // 128, 128, cols])
        for nm, (key, n) in VECS.items():
            W[(l, nm)] = E(f"l{l}_{nm}", [128, n // 128])
        W[(l, "gm_smwT")] = E(f"l{l}_gm_smwT", [GM_HEADS, 128, 128])
        W[(l, "gm_smb")] = E(f"l{l}_gm_smb", [1, GM_HEADS * 128])
        W[(l, "cw")] = E(f"l{l}_cw", [NPAIR, 3, 128, 4])
        W[(l, "al")] = E(f"l{l}_al", [NPAIR, 1, 1])
        W[(l, "db")] = E(f"l{l}_db", [NPAIR, 1, 1])
        W[(l, "nw")] = E(f"l{l}_nw", [128, 1])
    D = C.dram_tmp
    h = D("t_h", [D_FF, NTOK])
    x1_all = D("t_x1", [D_MODEL, NTC])
    puv = D("t_puv", [2048, NTOK])
    pdn_all = D("t_pdn", [PDN_ROWS, NTC])
    ycat_all = D("t_ycat", [D_MODEL, NTC])
    oq = D("t_oq", [NPAIR, 128, SEQ]); ok_ = D("t_ok", [NPAIR, 128, SEQ]); ov = D("t_ov", [NPAIR, 128, SEQ])
    ob = D("t_ob", [NPAIR, 1, SEQ]); og = D("t_og", [NPAIR, 1, SEQ])
    x2 = D("t_x2", [D_MODEL, NTOK]); q = D("t_q", [D_MODEL, NTOK]); o = D("t_o", [D_MODEL, NTOK]); x3 = D("t_x3", [D_MODEL, NTOK])
    kv = D("t_kv", [NB, 2 * D_MODEL, N_MEM])
    xmid = D("t_xmid", [D_MODEL, NTC])

    def gs(g):
        return slice(g * NTOK, (g + 1) * NTOK)

    for l in range(DEPTH):
        xin = x_all if l == 0 else xmid
        xout = xmid if l == 0 else out_all
        for g in range(G):
            C.begin_stage("", {"x": xin[:, gs(g)], "w": W[(l, "f1a_w")], "g": W[(l, "f1a_g")], "y": h})
            emit_lin(C, NTOK, D_MODEL, 2 * D_FF, True, "swiglu")
            C.begin_stage("", {"x": h, "w": W[(l, "f1b_w")], "pg": W[(l, "f1b_pg")], "res": xin[:, gs(g)], "y": x1_all[:, gs(g)]})
            emit_lin(C, NTOK, D_FF, D_MODEL, False, "none", post=0.5, NB=2)
            C.begin_stage("", {"x": x1_all[:, gs(g)], "w": W[(l, "ip_w")], "g": W[(l, "ip_g")], "y": RowSplit(puv, 2048, pdn_all[:, gs(g)])})
            emit_lin(C, NTOK, D_MODEL, PROJ_ROWS, True)
            C.begin_stage("", {"u": puv[0:1024, :], "v": puv[1024:2048, :], "smwT": W[(l, "gm_smwT")], "smb": W[(l, "gm_smb")],
                               "lng": W[(l, "gm_lng")], "lnb": W[(l, "gm_lnb")], "y": ycat_all[0:GM_WIDTH, gs(g)]})
            emit_gmlp(C)

        def bs(p):
            b = p // DN_HEADS
            return slice(b * SEQ, (b + 1) * SEQ)

        def hd(p):
            return p % DN_HEADS

        C.begin_stage("", {
            "qkv": _Sel(lambda k: pdn_all[k[1] * 1024 + hd(k[0]) * 128:k[1] * 1024 + (hd(k[0]) + 1) * 128, bs(k[0])]),
            "cw": W[(l, "cw")], "al": W[(l, "al")], "db": W[(l, "db")],
            "br": _Sel(lambda p: pdn_all[4096 + hd(p):4097 + hd(p), bs(p)]),
            "ar": _Sel(lambda p: pdn_all[4104 + hd(p):4105 + hd(p), bs(p)]),
            "oq": oq, "ok": ok_, "ov": ov, "ob": ob, "og": og})
        emit_dna(C, NP=NPAIR)
        C.begin_stage("", {
            "qT": oq, "kT": ok_, "ktok": None, "vtok": ov, "gcol": og, "bcol": ob, "brow": ob, "nw": W[(l, "nw")],
            "zT": _Sel(lambda p: pdn_all[3072 + hd(p) * 128:3072 + (hd(p) + 1) * 128, bs(p)]),
            "y": _Sel(lambda k: ycat_all[GM_WIDTH + hd(k[0]) * 128:GM_WIDTH + (hd(k[0]) + 1) * 128,
                                         slice(bs(k[0]).start + k[2].start, bs(k[0]).start + k[2].stop)])})
        emit_dnb(C, NP=NPAIR, ondev=True)
        for b in range(NB):
            C.begin_stage("", {"x": memT[b], "w": W[(l, "kv_w")], "g": W[(l, "kv_g")], "y": kv[b]})
            emit_lin(C, N_MEM, D_MODEL, 2 * D_MODEL, True, TT=256)
        for g in range(G):
            b = g // 2
            C.begin_stage("", {"x": ycat_all[:, gs(g)], "w": W[(l, "wo_w")], "pg": W[(l, "wo_pg")], "res": x1_all[:, gs(g)], "y": x2})
            emit_lin(C, NTOK, D_MODEL, D_MODEL, False, "none", post=1.0)
            C.begin_stage("", {"x": x2, "w": W[(l, "xq_w")], "g": W[(l, "xq_g")], "y": q})
            emit_lin(C, NTOK, D_MODEL, D_MODEL, True)
            C.begin_stage("", {"q": q, "k": kv[b][0:D_MODEL, :], "vT": kv[b][D_MODEL:2 * D_MODEL, :], "y": o})
            emit_xattn(C)
            C.begin_stage("", {"x": o, "w": W[(l, "xo_w")], "pg": W[(l, "xo_pg")], "res": x2, "y": x3})
            emit_lin(C, NTOK, D_MODEL, D_MODEL, False, "none", post=1.0)
            C.begin_stage("", {"x": x3, "w": W[(l, "f2a_w")], "g": W[(l, "f2a_g")], "y": h})
            emit_lin(C, NTOK, D_MODEL, 2 * D_FF, True, "swiglu")
            C.begin_stage("", {"x": h, "w": W[(l, "f2b_w")], "pg": W[(l, "f2b_pg")], "res": x3, "y": xout[:, gs(g)]})
            emit_lin(C, NTOK, D_FF, D_MODEL, False, "none", post=0.5, NB=2)
    return C.finish()


def kernel(**inputs):
    P = {k: np.asarray(v, dtype=np.float32) for k, v in inputs.items()}
    shared = {}
    for l in range(DEPTH):
        for st_, (key, rows, cols) in LIN_W.items():
            shared[f"l{l}_{st_}_w"] = w_layout(P[key][l], rows)
        for nm, (key, n) in VECS.items():
            shared[f"l{l}_{nm}"] = vec_layout(P[key][l])
        shared[f"l{l}_gm_smwT"] = np.ascontiguousarray(np.transpose(P["sm_w"][l], (0, 2, 1)))
        shared[f"l{l}_gm_smb"] = np.ascontiguousarray(P["sm_b"][l].reshape(1, -1))
        cw = np.empty((NPAIR, 3, 128, 4), np.float32)
        al = np.empty((NPAIR, 1, 1), np.float32)
        db = np.empty((NPAIR, 1, 1), np.float32)
        for p in range(NPAIR):
            hh = p % DN_HEADS
            for i in range(3):
                cw[p, i] = P["conv_w"][l][:, i * 1024 + hh * 128:i * 1024 + (hh + 1) * 128].T
            al[p, 0, 0] = P["a_log"][l][hh]
            db[p, 0, 0] = P["dt_bias"][l][hh]
        shared[f"l{l}_cw"], shared[f"l{l}_al"], shared[f"l{l}_db"] = cw, al, db
        shared[f"l{l}_nw"] = np.ascontiguousarray(P["dn_norm_w"][l].reshape(128, 1))
    maps = []
    for c in range(NUSE):
        xb = P["x"][c * NB:(c + 1) * NB].reshape(NB * SEQ, D_MODEL)
        m = dict(shared)
        m["x_all"] = np.ascontiguousarray(xb.T)
        m["memT"] = np.ascontiguousarray(np.transpose(P["mem"][c * NB:(c + 1) * NB], (0, 2, 1)))
        maps.append(m)
    if "full" not in _PROGS:
        _PROGS["full"] = prog_full()
    res = run_bass_kernel_spmd(_PROGS["full"], maps, core_ids=list(range(NUSE))).results
    out = np.empty((BATCH, SEQ, D_MODEL), np.float32)
    for c in range(NUSE):
        out[c * NB:(c + 1) * NB] = res[c]["out_all"].T.reshape(NB, SEQ, D_MODEL)
    return out
```
